# Optimizing a Trainium2 kernel written in Bass

```python
import jax, jax.numpy as jnp
from jax import lax
import numpy as np

D_MODEL = 2048
BATCH = 16
SEQ = 2048
DEPTH = 1

ATTN_HEADS = 8
HEAD_DIM = 128
ATTN_WIDTH = ATTN_HEADS * HEAD_DIM
CONV_WIDTH = D_MODEL - ATTN_WIDTH
IN_WIDTH = 3 * ATTN_WIDTH + 2 * CONV_WIDTH
CONV_KERNEL = 31
MOBA_BLOCK = 256
MOBA_TOPK = 3
Q_CHUNK = 8
ROPE_THETA = 10000.0
D_FF = -(-(8 * D_MODEL) // (3 * 256)) * 256
N_MOD = 6
EPS = 1e-6

kernel_name = "hymba_moba_conformer_adaln_block"


def rms_norm(x, g):
    xf = x.astype(jnp.float32)
    y = xf * lax.rsqrt(jnp.mean(xf * xf, axis=-1, keepdims=True) + EPS)
    return (y * g.astype(jnp.float32)).astype(x.dtype)


def modulate(h, shift, scale):
    return h * (1 + scale[:, None, :]) + shift[:, None, :]


def rope(x, pos):
    half = x.shape[-1] // 2
    inv = ROPE_THETA ** (-jnp.arange(half, dtype=jnp.float32) / half)
    ang = pos[:, None] * inv[None, :]
    cos, sin = jnp.cos(ang), jnp.sin(ang)
    xf = x.astype(jnp.float32)
    x1, x2 = xf[..., :half], xf[..., half:]
    out = jnp.concatenate([x1 * cos - x2 * sin, x2 * cos + x1 * sin], axis=-1)
    return out.astype(x.dtype)


def conv_module(a, b, w, bias, ln_g, ln_b):
    u = a * jax.nn.sigmoid(b)
    y = lax.conv_general_dilated(
        u, w[:, None, :].astype(u.dtype), window_strides=(1,),
        padding=[(CONV_KERNEL - 1, 0)],
        dimension_numbers=("NWC", "WIO", "NWC"),
        feature_group_count=CONV_WIDTH) + bias
    yf = y.astype(jnp.float32)
    mu = jnp.mean(yf, axis=-1, keepdims=True)
    var = jnp.mean(jnp.square(yf - mu), axis=-1, keepdims=True)
    yn = (yf - mu) * lax.rsqrt(var + EPS) * ln_g.astype(jnp.float32) + ln_b.astype(jnp.float32)
    return jax.nn.silu(yn).astype(a.dtype)


def moba_attention(q, k, v):
    B, H, T, D = q.shape
    nb = -(-T // MOBA_BLOCK)
    pad = nb * MOBA_BLOCK - T
    kp = jnp.pad(k, ((0, 0), (0, 0), (0, pad), (0, 0)))
    vp = jnp.pad(v, ((0, 0), (0, 0), (0, pad), (0, 0)))
    k_blocks = kp.reshape(B, H, nb, MOBA_BLOCK, D)
    v_blocks = vp.reshape(B, H, nb, MOBA_BLOCK, D)
    k_mean = jnp.mean(k_blocks.astype(jnp.float32), axis=3)
    topk = min(MOBA_TOPK, nb - 1)
    scale = HEAD_DIM ** -0.5
    neg = jnp.finfo(jnp.float32).min
    b_idx = jnp.arange(B)[:, None, None, None]
    h_idx = jnp.arange(H)[None, :, None, None]

    def chunk(start):
        qc = lax.dynamic_slice_in_dim(q, start, Q_CHUNK, axis=2)
        blk = start // MOBA_BLOCK
        q_pos = start + jnp.arange(Q_CHUNK)
        k_pos = blk * MOBA_BLOCK + jnp.arange(MOBA_BLOCK)
        k_own = lax.dynamic_index_in_dim(k_blocks, blk, axis=2, keepdims=False)
        v_own = lax.dynamic_index_in_dim(v_blocks, blk, axis=2, keepdims=False)
        s_own = jnp.einsum("bhqd,bhkd->bhqk", qc, k_own).astype(jnp.float32) * scale
        s_own = jnp.where(k_pos[None, :] <= q_pos[:, None], s_own, neg)
        if topk > 0:
            gate = jnp.einsum("bhqd,bhnd->bhqn", qc.astype(jnp.float32), k_mean)
            gate = jnp.where(jnp.arange(nb) < blk, gate, -jnp.inf)
            _, sel = lax.top_k(gate, topk)
            k_sel = k_blocks[b_idx, h_idx, sel]
            v_sel = v_blocks[b_idx, h_idx, sel]
            s_sel = jnp.einsum("bhqd,bhqnkd->bhqnk", qc, k_sel).astype(jnp.float32) * scale
            s_sel = jnp.where((sel < blk)[..., None], s_sel, neg)
            s_sel = s_sel.reshape(B, H, Q_CHUNK, topk * MOBA_BLOCK)
            p = jax.nn.softmax(jnp.concatenate([s_own, s_sel], axis=-1), axis=-1)
            p_own = p[..., :MOBA_BLOCK].astype(v.dtype)
            p_sel = p[..., MOBA_BLOCK:].reshape(B, H, Q_CHUNK, topk, MOBA_BLOCK).astype(v.dtype)
            out = (jnp.einsum("bhqk,bhkd->bhqd", p_own, v_own)
                   + jnp.einsum("bhqnk,bhqnkd->bhqd", p_sel, v_sel))
        else:
            p = jax.nn.softmax(s_own, axis=-1).astype(v.dtype)
            out = jnp.einsum("bhqk,bhkd->bhqd", p, v_own)
        return out.transpose(0, 2, 1, 3)

    starts = jnp.arange(0, T, Q_CHUNK, dtype=jnp.int32)
    outs = lax.map(chunk, starts)
    return outs.transpose(1, 0, 2, 3, 4).reshape(B, T, H * D)


def setup_inputs(seed: int = 0) -> dict:
    key = jax.random.key(seed)
    ks = jax.random.split(key, 16)
    f32 = jnp.float32
    n = lambda k, shape, s: jax.random.normal(k, shape, f32) * s
    L = DEPTH
    return {
        "x": n(ks[0], (BATCH, SEQ, D_MODEL), 1.0),
        "c": n(ks[1], (BATCH, D_MODEL), 1.0),
        "w_ada": n(ks[2], (L, D_MODEL, N_MOD * D_MODEL), 0.5 * D_MODEL ** -0.5),
        "b_ada": n(ks[3], (L, N_MOD * D_MODEL), 0.02),
        "g_mix": 1.0 + n(ks[4], (L, D_MODEL), 0.02),
        "w_in": n(ks[5], (L, D_MODEL, IN_WIDTH), D_MODEL ** -0.5),
        "conv_w": n(ks[6], (L, CONV_KERNEL, CONV_WIDTH), CONV_KERNEL ** -0.5),
        "conv_b": n(ks[7], (L, CONV_WIDTH), 0.02),
        "ln_g": 1.0 + n(ks[8], (L, CONV_WIDTH), 0.02),
        "ln_b": n(ks[9], (L, CONV_WIDTH), 0.02),
        "w_out": n(ks[10], (L, D_MODEL, D_MODEL), D_MODEL ** -0.5),
        "g_ffn": 1.0 + n(ks[11], (L, D_MODEL), 0.02),
        "w_gate": n(ks[12], (L, D_MODEL, D_FF), D_MODEL ** -0.5),
        "w_up": n(ks[13], (L, D_MODEL, D_FF), D_MODEL ** -0.5),
        "w_down": n(ks[14], (L, D_FF, D_MODEL), D_FF ** -0.5),
        "g_final": 1.0 + n(ks[15], (D_MODEL,), 0.02),
    }


def reference(x, c, w_ada, b_ada, g_mix, w_in, conv_w, conv_b, ln_g, ln_b,
              w_out, g_ffn, w_gate, w_up, w_down, g_final):
    B, T, _ = x.shape
    pos = jnp.arange(T, dtype=jnp.float32)
    cs = jax.nn.silu(c)
    splits = [ATTN_WIDTH, 2 * ATTN_WIDTH, 3 * ATTN_WIDTH, 3 * ATTN_WIDTH + CONV_WIDTH]
    h = x
    for l in range(DEPTH):
        mod = cs @ w_ada[l] + b_ada[l]
        sh_m, sc_m, gt_m, sh_f, sc_f, gt_f = jnp.split(mod, N_MOD, axis=-1)

        u = modulate(rms_norm(h, g_mix[l]), sh_m, sc_m)
        proj = u @ w_in[l]
        q, k, v, ga, gb = jnp.split(proj, splits, axis=-1)
        to_heads = lambda t: t.reshape(B, T, ATTN_HEADS, HEAD_DIM).transpose(0, 2, 1, 3)
        q = rope(to_heads(q), pos)
        k = rope(to_heads(k), pos)
        v = to_heads(v)
        attn_out = moba_attention(q, k, v)
        conv_out = conv_module(ga, gb, conv_w[l], conv_b[l], ln_g[l], ln_b[l])
        mixed = jnp.concatenate([attn_out, conv_out], axis=-1) @ w_out[l]
        h = h + gt_m[:, None, :] * mixed

        u = modulate(rms_norm(h, g_ffn[l]), sh_f, sc_f)
        ff = (jax.nn.silu(u @ w_gate[l]) * (u @ w_up[l])) @ w_down[l]
        h = h + gt_f[:, None, :] * ff
    return rms_norm(h, g_final)
```

```python
import numpy as np
import ml_dtypes
import concourse.bass as bass
import concourse.mybir as mybir
from concourse.bass_utils import run_bass_kernel_spmd

F32 = mybir.dt.float32
BF16 = mybir.dt.bfloat16
AF = mybir.ActivationFunctionType
ALU = mybir.AluOpType
AX = mybir.AxisListType

D = 2048
T = 2048
NH = 8
DFF = 5632
KC = 16
FC = 44
NTG = 4
EPS = 1e-6
BIG = 10000.0
SCALE = 128 ** -0.5
NCORES = 8


class Buf:
    __slots__ = ("name", "w", "r")

    def __init__(self, name):
        self.name = name
        self.w = {}
        self.r = {}


class Sched:
    ENGS = ("pe", "act", "dve", "pool", "sp")

    def __init__(self):
        self.streams = {e: [] for e in self.ENGS}
        self.count = {e: 0 for e in self.ENGS}
        self.seen = {e: {} for e in self.ENGS}
        self.snap = {}
        self.dma_keys = []

    def new_dma_sem(self, name):
        self.count[name] = 0
        self.dma_keys.append(name)
        return name

    def op(self, e, fn, reads=(), writes=(), dma=None):
        deps = {}

        def add(d, same_ok):
            for k, v in d.items():
                if k == e and not same_ok:
                    continue
                if deps.get(k, 0) < v:
                    deps[k] = v

        raw_same = e != "pe"
        for b in reads:
            add(b.w, raw_same)
        for b in writes:
            add(b.w, False)
            add(b.r, False)
        seen = self.seen[e]
        waits = []
        for k, v in deps.items():
            if seen.get(k, 0) >= v:
                continue
            waits.append((k, v))
            seen[k] = v
            sn = self.snap.get((k, v))
            if sn:
                for k2, v2 in sn.items():
                    if seen.get(k2, 0) < v2:
                        seen[k2] = v2
        if dma is None:
            key, amt = e, 1
        else:
            key, amt = dma, 16
        self.count[key] += amt
        val = self.count[key]
        self.snap[(key, val)] = dict(seen)
        self.streams[e].append((waits, fn, (key, amt)))
        for b in writes:
            b.w[key] = val
        for b in reads:
            b.r[key] = val

    def barrier(self):
        allkeys = [k for k in self.count if self.count[k] > 0]
        for e in self.ENGS:
            seen = self.seen[e]
            waits = []
            for k in allkeys:
                v = self.count[k]
                if k == e:
                    continue
                if seen.get(k, 0) < v:
                    waits.append((k, v))
                    seen[k] = v
            if waits:
                self.streams[e].append((waits, None, None))

    def final_wait(self, e="sp"):
        self.barrier()

    def emit(self, nc, block, sems):
        def make(ename):
            def body(eng):
                for waits, fn, inc in self.streams[ename]:
                    for k, v in waits:
                        eng.wait_ge(sems[k], v)
                    if fn is not None:
                        ins = fn(eng)
                        ins.then_inc(sems[inc[0]], inc[1])
            return body

        block.tensor(make("pe"))
        block.scalar(make("act"))
        block.vector(make("dve"))
        block.gpsimd(make("pool"))
        block.sync(make("sp"))


class Arena:
    def __init__(self, t, nbytes):
        self.t = t
        self.n = nbytes
        self.off = 0

    def mark(self):
        return self.off

    def reset(self, m):
        self.off = m

    def alloc(self, nbytes_per_part, dtype=BF16, shape=None):
        nb = (nbytes_per_part + 63) // 64 * 64
        assert self.off + nb <= self.n, f"SBUF arena overflow {self.off}+{nb}>{self.n}"
        a = self.off // 2
        v = self.t[:, a:a + nbytes_per_part // 2]
        self.off += nb
        if dtype == F32:
            v = v.bitcast(F32)
        return v


def build_program(dbg=None):
    nc = bass.Bass("TRN2", target_bir_lowering=False)
    S = Sched()

    def din(name, shape, dt=F32):
        return nc.dram_tensor(name, list(shape), dt, kind="ExternalInput").ap()

    x_d = din("x", [2, T, D])
    c_d = din("cT", [128, KC * 2])
    wada_d = din("wada", [96, 128, KC * 128])
    bada_d = din("bada", [128, 96])
    gmix_d = din("gmix", [128, KC])
    gffn_d = din("gffn", [128, KC])
    win_d = din("win", [40, 128, KC * 128])
    wout_d = din("wout", [16, 128, KC * 128])
    wg_d = din("wg", [FC, 128, KC * 128])
    wu_d = din("wu", [FC, 128, KC * 128])
    wd_d = din("wd", [16, 128, FC * 128])
    convw_d = din("convw", [128, 8 * 31])
    convb_d = din("convb", [128, 8])
    lng_d = din("lng", [128, 8])
    lnb_d = din("lnb", [128, 8])
    gfin_d = din("gfin", [128, D])
    identf_d = din("identf", [128, 128])
    identb_d = din("identb", [128, 128], BF16)
    onesb_d = din("onesb", [128, 128], BF16)
    cmask_d = din("cmask", [128, 4 * 512], BF16)
    ej_d = din("ej", [128, 8 * 128], BF16)
    cos_d = din("cos", [128, T])
    sin_d = din("sin", [128, T])
    past_d = din("pastb", [128, 16 * 8])
    own_d = din("ownfix", [128, 16 * 8])
    out_d = nc.dram_tensor("out", [2, T, D], F32, kind="ExternalOutput").ap()
    h1s_d = nc.dram_tensor("h1s", [2, T, D], F32, kind="Internal").ap()
    u2s_d = nc.dram_tensor("u2s", [2, NTG, 128, KC * 512], BF16, kind="Internal").ap()

    ARENA_BYTES = 207 * 1024
    arena_t = nc.alloc_sbuf_tensor("arena", [128, ARENA_BYTES // 2], BF16)
    AR = Arena(arena_t, ARENA_BYTES)

    banks = [nc.alloc_psum_tensor(f"bank{i}", [128, 512], F32) for i in range(8)]
    bank_buf = [Buf(f"bank{i}") for i in range(8)]

    def v3(ap, k):
        return ap.rearrange("p (k t) -> p k t", k=k)

    dma_sem_ctr = [0]

    def new_sem():
        dma_sem_ctr[0] += 1
        return S.new_dma_sem(f"dma{dma_sem_ctr[0]}")

    class Slot:
        def __init__(self, ap, name):
            self.ap = ap
            self.buf = Buf(name)
            self.sem = new_sem()

    def dma(queue, out_ap, in_ap, sem, reads=(), writes=()):
        S.op(queue, lambda eng, o=out_ap, i=in_ap: eng.dma_start(out=o, in_=i),
             reads=reads, writes=writes, dma=sem)

    def load_w(slot, src_ap):
        S.op("pool", lambda eng, o=slot.ap, i=src_ap: eng.dma_start(out=o, in_=i, max_dma_last_dim=8192),
             writes=[slot.buf], dma=slot.sem)

    def load_plain(queue, slot, src_ap, ap=None):
        dma(queue, slot.ap if ap is None else ap, src_ap, slot.sem, writes=[slot.buf])

    def mm(bank_i, out_ap, pairs, reads, first=True, last=True):
        n = len(pairs)

        def fn(eng):
            ins = None
            for i, (l, r) in enumerate(pairs):
                ins = eng.matmul(out_ap, l, r, start=(first and i == 0), stop=(last and i == n - 1))
            return ins

        S.op("pe", fn, reads=reads, writes=[bank_buf[bank_i]])

    def transposes(bank_i, items, reads):
        def fn(eng):
            ins = None
            for o, i, idn in items:
                ins = eng.transpose(o, i, idn)
            return ins

        S.op("pe", fn, reads=reads, writes=[bank_buf[bank_i]])

    def act(out, in_, func, reads, writes, bias=None, scale=None, accum=None):
        kw = {}
        if bias is not None:
            kw["bias"] = bias
        if scale is not None:
            kw["scale"] = scale
        if accum is not None:
            kw["accum_out"] = accum
        S.op("act", lambda eng: eng.activation(out, in_, func, **kw), reads=reads, writes=writes)

    def vop(e, name, reads, writes, *a, **kw):
        S.op(e, lambda eng: getattr(eng, name)(*a, **kw), reads=reads, writes=writes)

    def const_tile(nbytes, dtype, src, queue="sp"):
        ap = AR.alloc(nbytes, dtype)
        sl = Slot(ap, "const")
        load_plain(queue, sl, src)
        return sl

    identf = const_tile(512, F32, identf_d)
    identb = const_tile(256, BF16, identb_d)
    onesb = const_tile(256, BF16, onesb_d)
    cmask = const_tile(4 * 1024, BF16, cmask_d)
    ej = const_tile(8 * 256, BF16, ej_d)
    bada = const_tile(96 * 4, F32, bada_d)
    gmix = const_tile(KC * 4, F32, gmix_d)
    gffn = const_tile(KC * 4, F32, gffn_d)
    convw = const_tile(8 * 31 * 4, F32, convw_d)
    convb = const_tile(8 * 4, F32, convb_d)
    lng = const_tile(8 * 4, F32, lng_d)
    lnb = const_tile(8 * 4, F32, lnb_d)
    pastb = const_tile(16 * 8 * 4, F32, past_d)
    ownfix = const_tile(16 * 8 * 4, F32, own_d)
    cT = const_tile(KC * 2 * 4, F32, c_d)
    eps_ap = AR.alloc(4, F32)
    eps_buf = Buf("eps")
    vop("dve", "memset", [], [eps_buf], eps_ap, EPS)
    modT = AR.alloc(96 * 2 * 4, F32)
    modT_buf = Buf("modT")
    A1 = AR.alloc(KC * 2 * 4, F32)
    A2 = AR.alloc(KC * 2 * 4, F32)
    mod_buf = Buf("modcols")
    csT = AR.alloc(KC * 2 * 2, BF16)
    csT_buf = Buf("csT")
    stat = AR.alloc(64 * 4, F32)
    stat_bufs = [Buf(f"stat{i}") for i in range(64)]
    stat_ctr = [0]

    def new_stat():
        i = stat_ctr[0] % 64
        stat_ctr[0] += 1
        return stat[:, i:i + 1], stat_bufs[i]

    CONST_END = AR.mark()

    WBYTES = 22 * 1024
    w_region = AR.alloc(WBYTES, BF16)
    W_END = AR.mark()

    def make_wslots(nbytes, n):
        assert nbytes * n <= WBYTES
        return [Slot(w_region[:, i * nbytes // 2:(i + 1) * nbytes // 2], f"w{nbytes}_{i}") for i in range(n)]

    w4 = make_wslots(4096, 5)
    w11 = make_wslots(FC * 256, 2)
    w4_ctr = [0]
    w11_ctr = [0]

    def next_w4():
        s = w4[w4_ctr[0] % 5]
        w4_ctr[0] += 1
        return s

    def next_w11():
        s = w11[w11_ctr[0] % 2]
        w11_ctr[0] += 1
        return s

    act(csT, cT.ap, AF.Silu, [cT.buf], [csT_buf])
    csT3 = v3(csT, KC)
    modT3 = v3(modT, 96)
    modT_bufs = [Buf(f"modT{i}") for i in range(6)]
    A1_buf = Buf("A1")
    A2_buf = Buf("A2")
    A1_3 = v3(A1, KC)
    A2_3 = v3(A2, KC)

    def ada_chunk(n):
        sl = next_w4()
        load_w(sl, wada_d[n])
        w3 = v3(sl.ap, KC)
        bi = 2 + (n % 2)
        mm(bi, banks[bi][:, 0:2], [(w3[:, kc, :], csT3[:, kc, :]) for kc in range(KC)],
           reads=[sl.buf, csT_buf])
        act(modT3[:, n, :], banks[bi][:, 0:2], AF.Identity, [bank_buf[bi], bada.buf], [modT_bufs[n // 16]],
            bias=bada.ap[:, n:n + 1])

    def ada_A(which):
        A3, base, g, mb, ab = (A1_3, 16, gmix, modT_bufs[1], A1_buf) if which == 1 else \
            (A2_3, 64, gffn, modT_bufs[4], A2_buf)

        def f(eng):
            ins = None
            for kc in range(KC):
                ins = eng.tensor_scalar(A3[:, kc, :], modT3[:, base + kc, :], g.ap[:, kc:kc + 1], g.ap[:, kc:kc + 1],
                                        ALU.mult, ALU.add)
            return ins

        S.op("dve", f, reads=[mb, g.buf], writes=[ab])

    for n in range(32):
        ada_chunk(n)
    ada_A(1)
    ada_rest = list(range(32, 96))

    def colA(which, kc, b):
        return (A1_3 if which == 1 else A2_3)[:, kc, b:b + 1]

    def colB(which, kc, b):
        return modT3[:, (0 if which == 1 else 48) + kc, b:b + 1]

    def colG(which, n, b):
        return modT3[:, (32 if which == 1 else 80) + n, b:b + 1]

    def mod_reads(which):
        return [A1_buf, modT_bufs[0]] if which == 1 else [A2_buf, modT_bufs[3]]

    def norm_a1(src_ap, src_buf, junk_ap, junk_buf):
        ss, ssb = new_stat()
        act(junk_ap, src_ap, AF.Square, [src_buf], [junk_buf, ssb], accum=ss)
        sd, sdb = new_stat()
        act(sd, ss, AF.Sqrt, [ssb, eps_buf], [sdb], bias=eps_ap, scale=1.0 / D)
        rs, rsb = new_stat()
        vop("dve", "reciprocal", [sdb], [rsb], rs, sd)
        return rs, rsb

    def norm_a2(src_ap, src_buf, rs, rsb, xs_ap, xs_buf):
        vop("dve", "tensor_scalar", [src_buf, rsb], [xs_buf], xs_ap, src_ap, rs, None, ALU.mult)

    def norm_b(which, b, xs_ap, xs_buf, tr_banks, dst_fn, dst_buf):
        for half in range(2):
            bi = tr_banks[half]
            pb = banks[bi].bitcast(BF16)
            transposes(bi, [(pb[:, j * 128:(j + 1) * 128], xs_ap[:, (half * 8 + j) * 128:(half * 8 + j + 1) * 128],
                             identb.ap) for j in range(8)], reads=[xs_buf, identb.buf])

            def fea(eng, half=half, pb=pb):
                ins = None
                for j in range(0, 8, 2):
                    kc = half * 8 + j
                    ins = eng.activation(dst_fn(kc), pb[:, j * 128:(j + 1) * 128], AF.Identity,
                                         bias=colB(which, kc, b), scale=colA(which, kc, b))
                return ins

            def fed(eng, half=half, pb=pb):
                ins = None
                for j in range(1, 8, 2):
                    kc = half * 8 + j
                    ins = eng.tensor_scalar(dst_fn(kc), pb[:, j * 128:(j + 1) * 128], colA(which, kc, b),
                                            colB(which, kc, b), ALU.mult, ALU.add)
                return ins

            S.op("act", fea, reads=[bank_buf[bi]] + mod_reads(which), writes=[dst_buf])
            S.op("dve", fed, reads=[bank_buf[bi]] + mod_reads(which), writes=[dst_buf])

    PHASE_BASE = AR.mark()

    for s in range(2):
        S.barrier()
        AR.reset(PHASE_BASE)
        uT = AR.alloc(KC * T * 2, BF16)
        uT3 = v3(uT, KC)
        uT_bufs = [Buf(f"uT{tg}") for tg in range(NTG)]
        catT = AR.alloc(KC * T * 2, BF16)
        catT3 = v3(catT, KC)
        cat_bufs = [[Buf(f"cat{c}_{tg}") for tg in range(NTG)] for c in range(KC)]
        TMP_BASE = AR.mark()

        xt = [Slot(AR.alloc(D * 4, F32), f"xt{i}") for i in range(2)]
        xs = [(AR.alloc(D * 2, BF16), Buf(f"xs{i}")) for i in range(2)]
        junk1 = AR.alloc(D * 2, BF16)
        junk1_buf = Buf("junk1")
        st = {}

        def p1_load(tt):
            load_plain("sp", xt[tt % 2], x_d[s, tt * 128:(tt + 1) * 128, :])

        def p1_a1(tt):
            sl = xt[tt % 2]
            st[tt] = norm_a1(sl.ap, sl.buf, junk1, junk1_buf)

        def p1_a2(tt):
            sl = xt[tt % 2]
            rs, rsb = st[tt]
            norm_a2(sl.ap, sl.buf, rs, rsb, *xs[tt % 2])

        def p1_b(tt):
            norm_b(1, s, xs[tt % 2][0], xs[tt % 2][1], (0, 1),
                   lambda kc, tt=tt: uT3[:, kc, tt * 128:(tt + 1) * 128], uT_bufs[tt // 4])

        p1_load(0)
        p1_load(1)
        p1_a1(0)
        for tt in range(16):
            if tt + 1 < 16:
                p1_a1(tt + 1)
            p1_a2(tt)
            if tt + 2 < 16:
                p1_load(tt + 2)
            for _ in range(4):
                if ada_rest:
                    ada_chunk(ada_rest.pop(0))
            p1_b(tt)
        if s == 0:
            assert not ada_rest
            ada_A(2)

        def proj(bank_i, wslot, tg):
            w3 = v3(wslot.ap, KC)
            mm(bank_i, banks[bank_i][:, :], [(w3[:, kc, :], uT3[:, kc, tg * 512:(tg + 1) * 512]) for kc in range(KC)],
               reads=[wslot.buf, uT_bufs[tg]])

        AR.reset(TMP_BASE)
        diag = [(AR.alloc(31 * 128 * 2, BF16), Buf(f"diag{i}")) for i in range(2)]
        S1 = AR.alloc(T * 4, F32)
        S2 = AR.alloc(T * 4, F32)
        S_bufs = [Buf(f"S{tb}") for tb in range(NTG)]
        ysq = [(AR.alloc(512 * 2, BF16), Buf(f"ysq{i}")) for i in range(2)]
        sig = [(AR.alloc(512 * 4, F32), Buf(f"sig{i}")) for i in range(2)]
        ltmp = AR.alloc(512 * 4, F32)
        ltmp_buf = Buf("ltmp")
        ytmp = [(AR.alloc(512 * 4, F32), Buf(f"ytmp{i}")) for i in range(2)]
        convw3 = v3(convw.ap, 8)

        k = 0
        kq = 0
        pending = []

        def stats(c, tb, yq_ap, yq_buf):
            sl_ = slice(tb * 512, (tb + 1) * 512)
            mm(6, banks[6][:, :], [(onesb.ap, catT3[:, 8 + c, sl_])], reads=[cat_bufs[8 + c][tb], onesb.buf])
            mm(7, banks[7][:, :], [(onesb.ap, yq_ap)], reads=[yq_buf, onesb.buf])
            if c == 0:
                vop("dve", "tensor_copy", [bank_buf[6]], [S_bufs[tb]], S1[:, sl_], banks[6][:, :])
                vop("dve", "tensor_copy", [bank_buf[7]], [S_bufs[tb]], S2[:, sl_], banks[7][:, :])
            else:
                vop("dve", "tensor_tensor", [bank_buf[6], S_bufs[tb]], [S_bufs[tb]], S1[:, sl_], banks[6][:, :],
                    S1[:, sl_], ALU.add)
                vop("dve", "tensor_tensor", [bank_buf[7], S_bufs[tb]], [S_bufs[tb]], S2[:, sl_], banks[7][:, :],
                    S2[:, sl_], ALU.add)

        for c in range(8):
            wa = next_w4()
            load_w(wa, win_d[24 + c])
            wb = next_w4()
            load_w(wb, win_d[32 + c])
            for tg in range(NTG):
                ba, bb = 2 * (k % 2), 2 * (k % 2) + 1
                proj(ba, wa, tg)
                proj(bb, wb, tg)
                sg_ap, sg_buf = sig[k % 2]
                act(sg_ap, banks[bb][:, :], AF.Sigmoid, [bank_buf[bb]], [sg_buf])
                vop("dve", "tensor_tensor", [bank_buf[ba], sg_buf], [cat_bufs[8 + c][tg]],
                    catT3[:, 8 + c, tg * 512:(tg + 1) * 512], banks[ba][:, :], sg_ap, ALU.mult)
                k += 1
            dg, dg_buf = diag[c % 2]
            dg3 = v3(dg, 31)
            def fdv(eng, dg3=dg3, c=c):
                ins = None
                for tap in range(0, 31, 2):
                    ins = eng.tensor_scalar(dg3[:, tap, :], identf.ap, convw3[:, c, tap:tap + 1], None, ALU.mult)
                return ins

            def fac(eng, dg3=dg3, c=c):
                ins = None
                for tap in range(1, 31, 2):
                    ins = eng.activation(dg3[:, tap, :], identf.ap, AF.Copy, scale=convw3[:, c, tap:tap + 1])
                return ins

            S.op("dve", fdv, reads=[identf.buf, convw.buf], writes=[dg_buf])
            S.op("act", fac, reads=[identf.buf, convw.buf], writes=[dg_buf])
            for tb in range(NTG - 1, -1, -1):
                bi = 4 + (kq % 2)
                outs = []
                for tap in range(30, -1, -1):
                    sh = 30 - tap
                    lo = tb * 512 - sh
                    if lo >= 0:
                        outs.append((banks[bi][:, 0:512], dg3[:, tap, :], catT3[:, 8 + c, lo:lo + 512]))
                    else:
                        outs.append((banks[bi][:, sh:512], dg3[:, tap, :], catT3[:, 8 + c, 0:512 - sh]))

                def fnc(eng, outs=outs):
                    ins = None
                    for i, (o, l, r) in enumerate(outs):
                        ins = eng.matmul(o, l, r, start=(i == 0), stop=(i == len(outs) - 1))
                    return ins

                rd = [dg_buf, cat_bufs[8 + c][tb]] + ([cat_bufs[8 + c][tb - 1]] if tb > 0 else [])
                S.op("pe", fnc, reads=rd, writes=[bank_buf[bi]])
                if pending:
                    stats(*pending.pop())
                yq_ap, yq_buf = ysq[kq % 2]
                act(catT3[:, 8 + c, tb * 512:(tb + 1) * 512], banks[bi][:, :], AF.Identity,
                    [bank_buf[bi], convb.buf], [cat_bufs[8 + c][tb]], bias=convb.ap[:, c:c + 1])
                act(yq_ap, banks[bi][:, :], AF.Square, [bank_buf[bi], convb.buf], [yq_buf],
                    bias=convb.ap[:, c:c + 1])
                pending.append((c, tb, yq_ap, yq_buf))
                kq += 1
        stats(*pending.pop())
        for tb in range(NTG):
            sl_ = slice(tb * 512, (tb + 1) * 512)
            mu, var = S1[:, sl_], S2[:, sl_]
            sb_ = S_bufs[tb]
            vop("dve", "tensor_scalar", [sb_], [sb_], mu, mu, 1.0 / 1024, None, ALU.mult)
            vop("dve", "tensor_tensor", [sb_], [ltmp_buf], ltmp, mu, mu, ALU.mult)
            vop("dve", "scalar_tensor_tensor", [sb_, ltmp_buf], [sb_], var, var, 1.0 / 1024, ltmp,
                ALU.mult, ALU.subtract)
            act(var, var, AF.Sqrt, [sb_, eps_buf], [sb_], bias=eps_ap)
            vop("dve", "reciprocal", [sb_], [sb_], var, var)
            vop("dve", "scalar_tensor_tensor", [sb_], [sb_], mu, mu, -1.0, var, ALU.mult, ALU.mult)
            for c in range(8):
                yt_ap, yt_buf = ytmp[c % 2]
                vop("dve", "tensor_tensor", [cat_bufs[8 + c][tb], sb_], [yt_buf], yt_ap, catT3[:, 8 + c, sl_], var,
                    ALU.mult)
                vop("dve", "tensor_tensor", [yt_buf, sb_], [yt_buf], yt_ap, yt_ap, mu, ALU.add)
                act(catT3[:, 8 + c, sl_], yt_ap, AF.Silu, [yt_buf, lng.buf, lnb.buf],
                    [cat_bufs[8 + c][tb]], bias=lnb.ap[:, c:c + 1], scale=lng.ap[:, c:c + 1])

        S.barrier()
        AR.reset(TMP_BASE)
        cos_sl = Slot(AR.alloc(T * 4, F32), "cos")
        sin_sl = Slot(AR.alloc(T * 4, F32), "sin")
        load_plain("sp", cos_sl, cos_d)
        load_plain("sp", sin_sl, sin_d)
        qT = AR.alloc(T * 2, BF16)
        kT = AR.alloc(T * 2, BF16)
        qT_bufs = [Buf(f"qT{tg}") for tg in range(NTG)]
        kT_buf = Buf("kT")
        vsb = AR.alloc(16 * 128 * 2, BF16)
        vsb3 = v3(vsb, 16)
        v_buf = Buf("vsb")
        vT = [(AR.alloc(512 * 2, BF16), Buf(f"vT{i}")) for i in range(1)]
        rt1 = [(AR.alloc(512 * 4, F32), Buf(f"rt1_{i}")) for i in range(1)]
        rt2 = [(AR.alloc(512 * 4, F32), Buf(f"rt2_{i}")) for i in range(1)]
        pT = [(AR.alloc(512 * 2, BF16), Buf(f"pT{i}")) for i in range(3)]
        kmf = AR.alloc(8 * 4, F32)
        kmf_buf = Buf("kmf")
        kmb = AR.alloc(8 * 2, BF16)
        kmb_buf = Buf("kmb")
        g2 = AR.alloc(16 * 8 * 4, F32)
        g2_3 = v3(g2, 16)
        g2_buf = Buf("g2")
        top8 = AR.alloc(16 * 8 * 4, F32)
        top8_3 = v3(top8, 16)
        top8_buf = Buf("top8")
        selb = AR.alloc(16 * 8 * 4, F32)
        selb_3 = v3(selb, 16)
        selb_buf = Buf("selb")
        biasb = AR.alloc(16 * 8 * 2, BF16)
        biasb_3 = v3(biasb, 16)
        biasb_buf = Buf("biasb")
        biasT = [(AR.alloc(512 * 2, BF16), Buf(f"biasT{i}")) for i in range(2)]
        for bt_ap, bt_buf in biasT:
            vop("dve", "memset", [], [bt_buf], bt_ap, 0.0)
        rden = AR.alloc(512 * 4, F32)
        rden_buf = Buf("rden")

        def rope(bank_i, dst_ap, dst_buf, tg, k):
            ps = banks[bank_i]
            t1, t1b = rt1[0]
            t2, t2b = rt2[0]
            cs = cos_sl.ap[:, tg * 512:(tg + 1) * 512]
            sn = sin_sl.ap[:, tg * 512:(tg + 1) * 512]
            vop("dve", "tensor_tensor", [bank_buf[bank_i], cos_sl.buf], [t1b], t1, ps[:, :], cs, ALU.mult)
            vop("dve", "tensor_tensor", [bank_buf[bank_i], sin_sl.buf], [t2b], t2[0:64, :], ps[64:128, :],
                sn[64:128, :], ALU.mult)
            vop("dve", "tensor_tensor", [bank_buf[bank_i], sin_sl.buf], [t2b], t2[64:128, :], ps[0:64, :],
                sn[0:64, :], ALU.mult)
            vop("dve", "tensor_tensor", [t1b, t2b], [dst_buf], dst_ap, t1, t2, ALU.add)

        kk = 0
        for h in range(NH):
            wq = next_w4()
            load_w(wq, win_d[h])
            wk = next_w4()
            load_w(wk, win_d[8 + h])
            wv = next_w4()
            load_w(wv, win_d[16 + h])
            for tg in range(NTG):
                sl_ = slice(tg * 512, (tg + 1) * 512)
                bi = kk % 2
                proj(bi, wq, tg)
                rope(bi, qT[:, sl_], qT_bufs[tg], tg, kk)
                kk += 1
                bi = kk % 2
                proj(bi, wk, tg)
                rope(bi, kT[:, sl_], kT_buf, tg, kk)
                kk += 1
                bi = kk % 2
                proj(bi, wv, tg)
                vt_ap, vt_buf = vT[0]
                act(vt_ap, banks[bi][:, :], AF.Copy, [bank_buf[bi]], [vt_buf])
                kk += 1
                pb = banks[7].bitcast(BF16)
                transposes(7, [(pb[:, j * 128:(j + 1) * 128], vt_ap[:, j * 128:(j + 1) * 128], identb.ap)
                               for j in range(4)], reads=[vt_buf, identb.buf])
                vop("dve", "tensor_copy", [bank_buf[7]], [v_buf],
                    vsb3[:, tg * 4:(tg + 1) * 4, :], v3(pb[:, 0:512], 4))
            vop("dve", "tensor_reduce", [kT_buf], [kmf_buf], kmf, v3(kT, 8), AX.X, ALU.add)
            vop("dve", "tensor_scalar", [kmf_buf], [kmb_buf], kmb, kmf, 1.0 / 256, None, ALU.mult)
            g_ps = banks[7][:, 0:128]

            def fng(eng, g_ps=g_ps):
                ins = None
                for qt in range(16):
                    ins = eng.matmul(g_ps[:, qt * 8:(qt + 1) * 8], qT[:, qt * 128:(qt + 1) * 128], kmb,
                                     start=True, stop=True)
                return ins

            S.op("pe", fng, reads=qT_bufs + [kmb_buf], writes=[bank_buf[7]])
            vop("dve", "tensor_tensor", [bank_buf[7], pastb.buf], [g2_buf], g2, g_ps, pastb.ap, ALU.add)
            def fmax(eng):
                ins = None
                for qt in range(16):
                    ins = eng.max(top8_3[:, qt, :], g2_3[:, qt, :])
                return ins

            S.op("dve", fmax, reads=[g2_buf], writes=[top8_buf])
            vop("dve", "tensor_tensor", [g2_buf, top8_buf], [selb_buf], selb_3, g2_3,
                top8_3[:, :, 2:3].to_broadcast([128, 16, 8]), ALU.is_ge)
            vop("dve", "tensor_scalar", [selb_buf], [selb_buf], selb, selb, 1.0, BIG, ALU.subtract, ALU.mult)
            vop("dve", "tensor_tensor", [selb_buf, ownfix.buf], [biasb_buf], biasb, selb, ownfix.ap, ALU.max)

            for qg in range(NTG):
                bt_ap, bt_buf = biasT[qg % 2]
                pbt = banks[7].bitcast(BF16)
                transposes(7, [(pbt[0:8, j * 128:(j + 1) * 128], biasb_3[:, qg * 4 + j, :], identb.ap)
                               for j in range(4)], reads=[biasb_buf, identb.buf])
                vop("dve", "tensor_copy", [bank_buf[7]], [bt_buf], bt_ap[0:8, :], pbt[0:8, 0:512])
                nkt = 4 * qg + 4
                qsl = slice(qg * 512, (qg + 1) * 512)

                def qk(kt):
                    sb = 2 + (kt % 3)
                    pairs = [(kT[:, kt * 128:(kt + 1) * 128], qT[:, qsl]),
                             (v3(ej.ap, 8)[:, kt // 2, :], bt_ap)]
                    if kt >= 4 * qg:
                        pairs.append((identb.ap, v3(cmask.ap, 4)[:, kt - 4 * qg, :]))
                    mm(sb, banks[sb][:, :], pairs, reads=[kT_buf, qT_bufs[qg], bt_buf, ej.buf, identb.buf, cmask.buf])

                def pv(kt):
                    sb = 2 + (kt % 3)
                    p_ap, p_buf = pT[kt % 3]
                    act(p_ap, banks[sb][:, :], AF.Exp, [bank_buf[sb]], [p_buf], scale=SCALE)
                    mm(5, banks[5][:, :], [(vsb3[:, kt, :], p_ap)], reads=[v_buf, p_buf],
                       first=(kt == 0), last=(kt == nkt - 1))
                    mm(6, banks[6][:, :], [(onesb.ap, p_ap)], reads=[onesb.buf, p_buf],
                       first=(kt == 0), last=(kt == nkt - 1))

                qk(0)
                qk(1)
                for kt in range(nkt):
                    if kt + 2 < nkt:
                        qk(kt + 2)
                    pv(kt)
                vop("dve", "reciprocal", [bank_buf[6]], [rden_buf], rden, banks[6][:, :])
                vop("dve", "tensor_tensor", [bank_buf[5], rden_buf], [cat_bufs[h][qg]],
                    catT3[:, h, qsl], banks[5][:, :], rden, ALU.mult)

        S.barrier()
        AR.reset(PHASE_BASE)
        mgs = []
        for i in range(2):
            mg_ = AR.alloc(KC * 512 * 4, F32)
            mgs.append((v3(mg_, KC), [Buf(f"mg{i}_{n}") for n in range(KC)]))
        assert AR.mark() <= TMP_BASE, (AR.mark(), TMP_BASE)
        AR.reset(TMP_BASE)
        xt = [Slot(AR.alloc(D * 4, F32), f"xt4_{i}") for i in range(1)]
        h1t = [Slot(AR.alloc(D * 4, F32), f"h1t{i}") for i in range(2)]
        xs = [(AR.alloc(D * 2, BF16), Buf(f"xs4_{i}")) for i in range(2)]
        u2t = [Slot(AR.alloc(KC * 128 * 2, BF16), f"u2t{i}") for i in range(2)]
        junk4 = AR.alloc(D * 2, BF16)
        junk4_buf = Buf("junk4")

        def outproj_group(tg, n):
            mg3, mg_bufs = mgs[tg % 2]
            sl = next_w4()
            load_w(sl, wout_d[n])
            w3 = v3(sl.ap, KC)
            bi = n % 2
            mm(bi, banks[bi][:, :], [(w3[:, kc, :], catT3[:, kc, tg * 512:(tg + 1) * 512]) for kc in range(KC)],
               reads=[sl.buf] + [cat_bufs[kc][tg] for kc in range(KC)])
            act(mg3[:, n, :], banks[bi][:, :], AF.Copy, [bank_buf[bi], modT_bufs[2]], [mg_bufs[n]],
                scale=colG(1, n, s))

        for n in range(KC):
            outproj_group(0, n)
        k = 0
        for tg in range(NTG):
            mg3, mg_bufs = mgs[tg % 2]
            nxt = [(tg + 1, n) for n in range(KC)] if tg + 1 < NTG else []
            for j in range(4):
                tt = tg * 4 + j
                xsl = xt[0]
                load_plain("sp", xsl, x_d[s, tt * 128:(tt + 1) * 128, :])
                hsl = h1t[k % 2]
                for q4 in range(4):
                    bi = 2 + q4
                    transposes(bi, [(banks[bi][:, i * 128:(i + 1) * 128], mg3[:, q4 * 4 + i, j * 128:(j + 1) * 128],
                                     identf.ap) for i in range(4)],
                               reads=[mg_bufs[q4 * 4 + i] for i in range(4)] + [identf.buf])
                    vop("dve", "tensor_tensor", [bank_buf[bi], xsl.buf], [hsl.buf],
                        hsl.ap[:, q4 * 512:(q4 + 1) * 512], banks[bi][:, :], xsl.ap[:, q4 * 512:(q4 + 1) * 512], ALU.add)
                dma("sp", h1s_d[s, tt * 128:(tt + 1) * 128, :], hsl.ap, hsl.sem, reads=[hsl.buf])
                xs_ap, xs_buf = xs[k % 2]
                rs, rsb = norm_a1(hsl.ap, hsl.buf, junk4, junk4_buf)
                norm_a2(hsl.ap, hsl.buf, rs, rsb, xs_ap, xs_buf)
                for _ in range(4):
                    if nxt:
                        outproj_group(*nxt.pop(0))
                usl = u2t[k % 2]
                u3 = v3(usl.ap, KC)
                norm_b(2, s, xs_ap, xs_buf, (6, 7), lambda kc, u3=u3: u3[:, kc, :], usl.buf)
                dst = u2s_d[s, tg].rearrange("p (k t) -> p k t", k=KC)[:, :, j * 128:(j + 1) * 128]
                dma("sp", dst, u3, usl.sem, reads=[usl.buf])
                k += 1

        S.barrier()
        AR.reset(PHASE_BASE)
        hT = AR.alloc(FC * 1024 * 2, BF16)
        hT3 = v3(hT, FC)
        hT_bufs = [[Buf(f"hT{f}_{t2}") for t2 in range(2)] for f in range(FC)]
        gfin = Slot(AR.alloc(D * 4, F32), "gfin")
        load_plain("sp", gfin, gfin_d)
        Y_BASE = AR.mark()
        for g in range(2):
            S.barrier()
            AR.reset(Y_BASE)
            u2g = [Slot(AR.alloc(KC * 512 * 2, BF16), f"u2g{i}") for i in range(2)]
            sgt = [(AR.alloc(512 * 4, F32), Buf(f"sgt{i}")) for i in range(2)]
            for t2 in range(2):
                load_plain("sp", u2g[t2], u2s_d[s, g * 2 + t2])
            k = 0
            for f in range(FC):
                wgs = next_w4()
                load_w(wgs, wg_d[f])
                wus = next_w4()
                load_w(wus, wu_d[f])
                wg3 = v3(wgs.ap, KC)
                wu3 = v3(wus.ap, KC)
                for t2 in range(2):
                    u3 = v3(u2g[t2].ap, KC)
                    bg, bu = 2 * (k % 2), 2 * (k % 2) + 1
                    mm(bg, banks[bg][:, :], [(wg3[:, kc, :], u3[:, kc, :]) for kc in range(KC)],
                       reads=[wgs.buf, u2g[t2].buf])
                    mm(bu, banks[bu][:, :], [(wu3[:, kc, :], u3[:, kc, :]) for kc in range(KC)],
                       reads=[wus.buf, u2g[t2].buf])
                    sg_ap, sg_buf = sgt[k % 2]
                    act(sg_ap, banks[bg][:, :], AF.Silu, [bank_buf[bg]], [sg_buf])
                    vop("dve", "tensor_tensor", [bank_buf[bu], sg_buf], [hT_bufs[f][t2]],
                        hT3[:, f, t2 * 512:(t2 + 1) * 512], banks[bu][:, :], sg_ap, ALU.mult)
                    k += 1
            S.barrier()
            AR.reset(Y_BASE)
            fg = AR.alloc(KC * 512 * 4, F32)
            fg3 = v3(fg, KC)
            fg_bufs = [Buf(f"fg{n}") for n in range(KC)]
            h1l = [Slot(AR.alloc(D * 4, F32), f"h1l{i}") for i in range(2)]
            junk = AR.alloc(D * 2, BF16)
            junk_buf = Buf("junk")
            k = 0
            for t2 in range(2):
                for n in range(KC):
                    sl = next_w11()
                    load_w(sl, wd_d[n])
                    w3 = v3(sl.ap, FC)
                    bi = n % 2
                    mm(bi, banks[bi][:, :], [(w3[:, f, :], hT3[:, f, t2 * 512:(t2 + 1) * 512]) for f in range(FC)],
                       reads=[sl.buf] + [hT_bufs[f][t2] for f in range(FC)])
                    act(fg3[:, n, :], banks[bi][:, :], AF.Copy, [bank_buf[bi], modT_bufs[5]], [fg_bufs[n]],
                        scale=colG(2, n, s))
                for j in range(4):
                    tt = g * 8 + t2 * 4 + j
                    hsl = h1l[k % 2]
                    load_plain("sp", hsl, h1s_d[s, tt * 128:(tt + 1) * 128, :])
                    for q4 in range(4):
                        bi = 2 + q4
                        transposes(bi, [(banks[bi][:, i * 128:(i + 1) * 128],
                                         fg3[:, q4 * 4 + i, j * 128:(j + 1) * 128], identf.ap) for i in range(4)],
                                   reads=[fg_bufs[q4 * 4 + i] for i in range(4)] + [identf.buf])
                        vop("dve", "tensor_tensor", [bank_buf[bi], hsl.buf], [hsl.buf],
                            hsl.ap[:, q4 * 512:(q4 + 1) * 512], banks[bi][:, :], hsl.ap[:, q4 * 512:(q4 + 1) * 512],
                            ALU.add)
                    ss, ssb = new_stat()
                    act(junk, hsl.ap, AF.Square, [hsl.buf], [junk_buf, ssb], accum=ss)
                    sd, sdb = new_stat()
                    act(sd, ss, AF.Sqrt, [ssb, eps_buf], [sdb], bias=eps_ap, scale=1.0 / D)
                    rs, rsb = new_stat()
                    vop("dve", "reciprocal", [sdb], [rsb], rs, sd)
                    vop("dve", "scalar_tensor_tensor", [hsl.buf, rsb, gfin.buf], [hsl.buf],
                        hsl.ap, hsl.ap, rs, gfin.ap, ALU.mult, ALU.mult)
                    dma("sp", out_d[s, tt * 128:(tt + 1) * 128, :], hsl.ap, hsl.sem, reads=[hsl.buf])
                    k += 1

    S.barrier()

    sem_names = list(S.ENGS) + S.dma_keys
    sems = {}
    from contextlib import ExitStack
    with ExitStack() as es:
        for nm in sem_names:
            sems[nm] = es.enter_context(nc.semaphore(nm))
        block = es.enter_context(nc.Block())
        S.emit(nc, block, sems)
    return nc


def _tile_w(w, kc):
    K, N = w.shape
    return np.ascontiguousarray(w.reshape(kc, 128, N // 128, 128).transpose(2, 1, 0, 3)).reshape(N // 128, 128, kc * 128)


def _cols(v, n):
    return np.ascontiguousarray(v.reshape(n, 128).T)


_CACHE = {}


def _constants():
    if "c" in _CACHE:
        return _CACHE["c"]
    bf = ml_dtypes.bfloat16
    identf = np.eye(128, dtype=np.float32)
    identb = identf.astype(bf)
    onesb = np.ones((128, 128), bf)
    k = np.arange(128)[:, None]
    q = np.arange(512)[None, :]
    cm = np.stack([np.where(j * 128 + k > q, -BIG, 0.0) for j in range(4)], 1).astype(np.float32)
    cmask = cm.reshape(128, 4 * 512).astype(bf)
    ej = np.zeros((128, 8, 128), np.float32)
    for j in range(8):
        ej[j, j, :] = 1.0
    ej = ej.reshape(128, 8 * 128).astype(bf)
    half = 64
    inv = (np.float32(10000.0) ** (-np.arange(half, dtype=np.float32) / np.float32(half))).astype(np.float32)
    pos = np.arange(T, dtype=np.float32)
    ang = (pos[:, None] * inv[None, :]).astype(np.float32)
    cos = np.cos(ang).astype(np.float32).T
    sin = np.sin(ang).astype(np.float32).T
    cosF = np.concatenate([cos, cos], 0)
    sinS = np.concatenate([sin, -sin], 0)
    past = np.zeros((128, 16, 8), np.float32)
    own = np.full((128, 16, 8), -BIG, np.float32)
    for qt in range(16):
        blk = qt // 2
        past[:, qt, blk:] = -1e30
        own[:, qt, blk] = 0.0
    c = dict(identf=identf, identb=identb, onesb=onesb, cmask=cmask, ej=ej,
             cos=np.ascontiguousarray(cosF), sin=np.ascontiguousarray(sinS),
             pastb=past.reshape(128, 128), ownfix=own.reshape(128, 128))
    _CACHE["c"] = c
    return c


def kernel(x, c, w_ada, b_ada, g_mix, w_in, conv_w, conv_b, ln_g, ln_b, w_out, g_ffn, w_gate, w_up, w_down,
           g_final):
    f = lambda a: np.asarray(a, dtype=np.float32)
    x = f(x)
    c = f(c)
    shared = dict(
        wada=_tile_w(f(w_ada)[0], KC),
        bada=_cols(f(b_ada)[0], 96),
        gmix=_cols(f(g_mix)[0], KC),
        gffn=_cols(f(g_ffn)[0], KC),
        win=_tile_w(f(w_in)[0], KC),
        wout=_tile_w(f(w_out)[0], KC),
        wg=_tile_w(f(w_gate)[0], KC),
        wu=_tile_w(f(w_up)[0], KC),
        wd=_tile_w(f(w_down)[0], FC),
        convw=np.ascontiguousarray(f(conv_w)[0].reshape(31, 8, 128).transpose(2, 1, 0)).reshape(128, 8 * 31),
        convb=_cols(f(conv_b)[0], 8),
        lng=_cols(f(ln_g)[0], 8),
        lnb=_cols(f(ln_b)[0], 8),
        gfin=np.ascontiguousarray(np.broadcast_to(f(g_final)[None, :], (128, D))),
    )
    shared.update(_constants())
    in_maps = []
    for i in range(NCORES):
        m = dict(shared)
        m["x"] = np.ascontiguousarray(x[2 * i:2 * i + 2])
        m["cT"] = np.ascontiguousarray(c[2 * i:2 * i + 2].reshape(2, KC, 128).transpose(2, 1, 0)).reshape(128, KC * 2)
        in_maps.append(m)
    if "nc" not in _CACHE:
        _CACHE["nc"] = build_program()
    nc = _CACHE["nc"]
    res = run_bass_kernel_spmd(nc, in_maps, core_ids=list(range(NCORES)))
    out = np.concatenate([np.asarray(r["out"]) for r in res.results], axis=0)
    return out.astype(np.float32)
```

```python
import numpy as np
import ml_dtypes
import concourse.bass as bass
import concourse.mybir as mybir
from concourse.bass_utils import run_bass_kernel_spmd

F32 = mybir.dt.float32
BF16 = mybir.dt.bfloat16
AF = mybir.ActivationFunctionType
ALU = mybir.AluOpType
AX = mybir.AxisListType

D = 2048
T = 2048
NH = 8
DFF = 5632
KC = 16
FC = 44
NTG = 4
EPS = 1e-6
BIG = 10000.0
SCALE = 128 ** -0.5
NCORES = 8


class Buf:
    __slots__ = ("name", "w", "r")

    def __init__(self, name):
        self.name = name
        self.w = {}
        self.r = {}


class Sched:
    ENGS = ("pe", "act", "dve", "pool", "sp")

    def __init__(self):
        self.streams = {e: [] for e in self.ENGS}
        self.count = {e: 0 for e in self.ENGS}
        self.seen = {e: {} for e in self.ENGS}
        self.snap = {}
        self.dma_keys = []

    def new_dma_sem(self, name):
        self.count[name] = 0
        self.dma_keys.append(name)
        return name

    def op(self, e, fn, reads=(), writes=(), dma=None):
        deps = {}

        def add(d, same_ok):
            for k, v in d.items():
                if k == e and not same_ok:
                    continue
                if deps.get(k, 0) < v:
                    deps[k] = v

        raw_same = e != "pe"
        for b in reads:
            add(b.w, raw_same)
        for b in writes:
            add(b.w, False)
            add(b.r, False)
        seen = self.seen[e]
        waits = []
        for k, v in deps.items():
            if seen.get(k, 0) >= v:
                continue
            waits.append((k, v))
            seen[k] = v
            sn = self.snap.get((k, v))
            if sn:
                for k2, v2 in sn.items():
                    if seen.get(k2, 0) < v2:
                        seen[k2] = v2
        if dma is None:
            key, amt = e, 1
        else:
            key, amt = dma, 16
        self.count[key] += amt
        val = self.count[key]
        self.snap[(key, val)] = dict(seen)
        self.streams[e].append((waits, fn, (key, amt)))
        for b in writes:
            b.w[key] = val
        for b in reads:
            b.r[key] = val

    def barrier(self):
        allkeys = [k for k in self.count if self.count[k] > 0]
        for e in self.ENGS:
            seen = self.seen[e]
            waits = []
            for k in allkeys:
                v = self.count[k]
                if k == e:
                    continue
                if seen.get(k, 0) < v:
                    waits.append((k, v))
                    seen[k] = v
            if waits:
                self.streams[e].append((waits, None, None))

    def final_wait(self, e="sp"):
        self.barrier()

    def emit(self, nc, block, sems):
        def make(ename):
            def body(eng):
                for waits, fn, inc in self.streams[ename]:
                    for k, v in waits:
                        eng.wait_ge(sems[k], v)
                    if fn is not None:
                        ins = fn(eng)
                        ins.then_inc(sems[inc[0]], inc[1])
            return body

        block.tensor(make("pe"))
        block.scalar(make("act"))
        block.vector(make("dve"))
        block.gpsimd(make("pool"))
        block.sync(make("sp"))


class Arena:
    def __init__(self, t, nbytes):
        self.t = t
        self.n = nbytes
        self.off = 0

    def mark(self):
        return self.off

    def reset(self, m):
        self.off = m

    def alloc(self, nbytes_per_part, dtype=BF16, shape=None):
        nb = (nbytes_per_part + 63) // 64 * 64
        assert self.off + nb <= self.n, f"SBUF arena overflow {self.off}+{nb}>{self.n}"
        a = self.off // 2
        v = self.t[:, a:a + nbytes_per_part // 2]
        self.off += nb
        if dtype == F32:
            v = v.bitcast(F32)
        return v


def build_program(dbg=None):
    nc = bass.Bass("TRN2", target_bir_lowering=False)
    S = Sched()

    def din(name, shape, dt=F32):
        return nc.dram_tensor(name, list(shape), dt, kind="ExternalInput").ap()

    x_d = din("x", [2, T, D])
    c_d = din("cT", [128, KC * 2])
    wada_d = din("wada", [96, 128, KC * 128])
    bada_d = din("bada", [128, 96])
    gmix_d = din("gmix", [128, KC])
    gffn_d = din("gffn", [128, KC])
    win_d = din("win", [40, 128, KC * 128])
    wout_d = din("wout", [16, 128, KC * 128])
    wg_d = din("wg", [FC, 128, KC * 128])
    wu_d = din("wu", [FC, 128, KC * 128])
    wd_d = din("wd", [16, 128, FC * 128])
    convw_d = din("convw", [128, 8 * 31])
    convb_d = din("convb", [128, 8])
    lng_d = din("lng", [128, 8])
    lnb_d = din("lnb", [128, 8])
    gfin_d = din("gfin", [128, D])
    identf_d = din("identf", [128, 128])
    identb_d = din("identb", [128, 128], BF16)
    onesb_d = din("onesb", [128, 128], BF16)
    cmask_d = din("cmask", [128, 4 * 512], BF16)
    ej_d = din("ej", [128, 8 * 128], BF16)
    cos_d = din("cos", [128, T])
    sin_d = din("sin", [128, T])
    past_d = din("pastb", [128, 16 * 8])
    own_d = din("ownfix", [128, 16 * 8])
    out_d = nc.dram_tensor("out", [2, T, D], F32, kind="ExternalOutput").ap()
    h1s_d = nc.dram_tensor("h1s", [2, T, D], F32, kind="Internal").ap()
    u2s_d = nc.dram_tensor("u2s", [2, NTG, 128, 4 * KC * 128], BF16, kind="Internal").ap()

    ARENA_BYTES = 207 * 1024
    arena_t = nc.alloc_sbuf_tensor("arena", [128, ARENA_BYTES // 2], BF16)
    AR = Arena(arena_t, ARENA_BYTES)

    banks = [nc.alloc_psum_tensor(f"bank{i}", [128, 512], F32) for i in range(8)]
    bank_buf = [Buf(f"bank{i}") for i in range(8)]

    def v3(ap, k):
        return ap.rearrange("p (k t) -> p k t", k=k)

    dma_sem_ctr = [0]

    def new_sem():
        dma_sem_ctr[0] += 1
        return S.new_dma_sem(f"dma{dma_sem_ctr[0]}")

    class Slot:
        def __init__(self, ap, name):
            self.ap = ap
            self.buf = Buf(name)
            self.sem = new_sem()

    def dma(queue, out_ap, in_ap, sem, reads=(), writes=()):
        S.op(queue, lambda eng, o=out_ap, i=in_ap: eng.dma_start(out=o, in_=i),
             reads=reads, writes=writes, dma=sem)

    def load_w(slot, src_ap):
        S.op("pool", lambda eng, o=slot.ap, i=src_ap: eng.dma_start(out=o, in_=i, max_dma_last_dim=8192),
             writes=[slot.buf], dma=slot.sem)

    def load_plain(queue, slot, src_ap, ap=None):
        dma(queue, slot.ap if ap is None else ap, src_ap, slot.sem, writes=[slot.buf])

    def mm(bank_i, out_ap, pairs, reads, first=True, last=True):
        n = len(pairs)

        def fn(eng):
            ins = None
            for i, (l, r) in enumerate(pairs):
                ins = eng.matmul(out_ap, l, r, start=(first and i == 0), stop=(last and i == n - 1))
            return ins

        S.op("pe", fn, reads=reads, writes=[bank_buf[bank_i]])

    def transposes(bank_i, items, reads):
        def fn(eng):
            ins = None
            for o, i, idn in items:
                ins = eng.transpose(o, i, idn)
            return ins

        S.op("pe", fn, reads=reads, writes=[bank_buf[bank_i]])

    def act(out, in_, func, reads, writes, bias=None, scale=None, accum=None):
        kw = {}
        if bias is not None:
            kw["bias"] = bias
        if scale is not None:
            kw["scale"] = scale
        if accum is not None:
            kw["accum_out"] = accum
        S.op("act", lambda eng: eng.activation(out, in_, func, **kw), reads=reads, writes=writes)

    def vop(e, name, reads, writes, *a, **kw):
        S.op(e, lambda eng: getattr(eng, name)(*a, **kw), reads=reads, writes=writes)

    def const_tile(nbytes, dtype, src, queue="sp"):
        ap = AR.alloc(nbytes, dtype)
        sl = Slot(ap, "const")
        load_plain(queue, sl, src)
        return sl

    identf = const_tile(512, F32, identf_d)
    identb = const_tile(256, BF16, identb_d)
    onesb = const_tile(256, BF16, onesb_d)
    cmask = const_tile(4 * 1024, BF16, cmask_d)
    ej = const_tile(8 * 256, BF16, ej_d)
    bada = const_tile(96 * 4, F32, bada_d)
    gmix = const_tile(KC * 4, F32, gmix_d)
    gffn = const_tile(KC * 4, F32, gffn_d)
    convw = const_tile(8 * 31 * 4, F32, convw_d)
    convb = const_tile(8 * 4, F32, convb_d)
    lng = const_tile(8 * 4, F32, lng_d)
    lnb = const_tile(8 * 4, F32, lnb_d)
    pastb = const_tile(16 * 8 * 4, F32, past_d)
    ownfix = const_tile(16 * 8 * 4, F32, own_d)
    cT = const_tile(KC * 2 * 4, F32, c_d)
    eps_ap = AR.alloc(4, F32)
    eps_buf = Buf("eps")
    vop("dve", "memset", [], [eps_buf], eps_ap, EPS)
    modT = AR.alloc(96 * 2 * 4, F32)
    modT_buf = Buf("modT")
    A1 = AR.alloc(KC * 2 * 4, F32)
    A2 = AR.alloc(KC * 2 * 4, F32)
    mod_buf = Buf("modcols")
    csT = AR.alloc(KC * 2 * 2, BF16)
    csT_buf = Buf("csT")
    stat = AR.alloc(64 * 4, F32)
    stat_bufs = [Buf(f"stat{i}") for i in range(64)]
    stat_ctr = [0]

    def new_stat():
        i = stat_ctr[0] % 64
        stat_ctr[0] += 1
        return stat[:, i:i + 1], stat_bufs[i]

    CONST_END = AR.mark()

    WBYTES = 22 * 1024
    w_region = AR.alloc(WBYTES, BF16)
    W_END = AR.mark()

    def make_wslots(nbytes, n):
        assert nbytes * n <= WBYTES
        return [Slot(w_region[:, i * nbytes // 2:(i + 1) * nbytes // 2], f"w{nbytes}_{i}") for i in range(n)]

    w4 = make_wslots(4096, 5)
    w11 = make_wslots(FC * 256, 2)
    w4_ctr = [0]
    w11_ctr = [0]

    def next_w4():
        s = w4[w4_ctr[0] % 5]
        w4_ctr[0] += 1
        return s

    def next_w11():
        s = w11[w11_ctr[0] % 2]
        w11_ctr[0] += 1
        return s

    act(csT, cT.ap, AF.Silu, [cT.buf], [csT_buf])
    csT3 = v3(csT, KC)
    modT3 = v3(modT, 96)
    modT_bufs = [Buf(f"modT{i}") for i in range(6)]
    A1_buf = Buf("A1")
    A2_buf = Buf("A2")
    A1_3 = v3(A1, KC)
    A2_3 = v3(A2, KC)

    def ada_chunk(n):
        sl = next_w4()
        load_w(sl, wada_d[n])
        w3 = v3(sl.ap, KC)
        bi = 2 + (n % 2)
        mm(bi, banks[bi][:, 0:2], [(w3[:, kc, :], csT3[:, kc, :]) for kc in range(KC)],
           reads=[sl.buf, csT_buf])
        act(modT3[:, n, :], banks[bi][:, 0:2], AF.Identity, [bank_buf[bi], bada.buf], [modT_bufs[n // 16]],
            bias=bada.ap[:, n:n + 1])

    def ada_A(which):
        A3, base, g, mb, ab = (A1_3, 16, gmix, modT_bufs[1], A1_buf) if which == 1 else \
            (A2_3, 64, gffn, modT_bufs[4], A2_buf)

        def f(eng):
            ins = None
            for kc in range(KC):
                ins = eng.tensor_scalar(A3[:, kc, :], modT3[:, base + kc, :], g.ap[:, kc:kc + 1], g.ap[:, kc:kc + 1],
                                        ALU.mult, ALU.add)
            return ins

        S.op("dve", f, reads=[mb, g.buf], writes=[ab])

    for n in range(32):
        ada_chunk(n)
    ada_A(1)
    ada_rest = list(range(32, 96))

    def colA(which, kc, b):
        return (A1_3 if which == 1 else A2_3)[:, kc, b:b + 1]

    def colB(which, kc, b):
        return modT3[:, (0 if which == 1 else 48) + kc, b:b + 1]

    def colG(which, n, b):
        return modT3[:, (32 if which == 1 else 80) + n, b:b + 1]

    def mod_reads(which):
        return [A1_buf, modT_bufs[0]] if which == 1 else [A2_buf, modT_bufs[3]]

    def norm_a1(src_ap, src_buf, junk_ap, junk_buf):
        ss, ssb = new_stat()
        act(junk_ap, src_ap, AF.Square, [src_buf], [junk_buf, ssb], accum=ss)
        sd, sdb = new_stat()
        act(sd, ss, AF.Sqrt, [ssb, eps_buf], [sdb], bias=eps_ap, scale=1.0 / D)
        rs, rsb = new_stat()
        vop("dve", "reciprocal", [sdb], [rsb], rs, sd)
        return rs, rsb

    def norm_a2(src_ap, src_buf, rs, rsb, xs_ap, xs_buf):
        vop("dve", "tensor_scalar", [src_buf, rsb], [xs_buf], xs_ap, src_ap, rs, None, ALU.mult)

    def norm_b(which, b, xs_ap, xs_buf, tr_banks, dst_fn, dst_buf):
        for half in range(2):
            bi = tr_banks[half]
            pb = banks[bi].bitcast(BF16)
            transposes(bi, [(pb[:, j * 128:(j + 1) * 128], xs_ap[:, (half * 8 + j) * 128:(half * 8 + j + 1) * 128],
                             identb.ap) for j in range(8)], reads=[xs_buf, identb.buf])

            def fea(eng, half=half, pb=pb):
                ins = None
                for j in range(0, 8, 2):
                    kc = half * 8 + j
                    ins = eng.activation(dst_fn(kc), pb[:, j * 128:(j + 1) * 128], AF.Identity,
                                         bias=colB(which, kc, b), scale=colA(which, kc, b))
                return ins

            def fed(eng, half=half, pb=pb):
                ins = None
                for j in range(1, 8, 2):
                    kc = half * 8 + j
                    ins = eng.tensor_scalar(dst_fn(kc), pb[:, j * 128:(j + 1) * 128], colA(which, kc, b),
                                            colB(which, kc, b), ALU.mult, ALU.add)
                return ins

            S.op("act", fea, reads=[bank_buf[bi]] + mod_reads(which), writes=[dst_buf])
            S.op("dve", fed, reads=[bank_buf[bi]] + mod_reads(which), writes=[dst_buf])

    PHASE_BASE = AR.mark()

    for s in range(2):
        S.barrier()
        AR.reset(PHASE_BASE)
        uT = AR.alloc(KC * T * 2, BF16)
        uT3 = v3(uT, KC)
        uT_bufs = [Buf(f"uT{tg}") for tg in range(NTG)]
        catT = AR.alloc(KC * T * 2, BF16)
        catT3 = v3(catT, KC)
        cat_bufs = [[Buf(f"cat{c}_{tg}") for tg in range(NTG)] for c in range(KC)]
        TMP_BASE = AR.mark()

        xt = [Slot(AR.alloc(D * 4, F32), f"xt{i}") for i in range(2)]
        xs = [(AR.alloc(D * 2, BF16), Buf(f"xs{i}")) for i in range(2)]
        junk1 = AR.alloc(D * 2, BF16)
        junk1_buf = Buf("junk1")
        st = {}

        def p1_load(tt):
            load_plain("sp", xt[tt % 2], x_d[s, tt * 128:(tt + 1) * 128, :])

        def p1_a1(tt):
            sl = xt[tt % 2]
            st[tt] = norm_a1(sl.ap, sl.buf, junk1, junk1_buf)

        def p1_a2(tt):
            sl = xt[tt % 2]
            rs, rsb = st[tt]
            norm_a2(sl.ap, sl.buf, rs, rsb, *xs[tt % 2])

        def p1_b(tt):
            norm_b(1, s, xs[tt % 2][0], xs[tt % 2][1], (0, 1),
                   lambda kc, tt=tt: uT3[:, kc, tt * 128:(tt + 1) * 128], uT_bufs[tt // 4])

        p1_load(0)
        p1_load(1)
        p1_a1(0)
        for tt in range(16):
            if tt + 1 < 16:
                p1_a1(tt + 1)
            p1_a2(tt)
            if tt + 2 < 16:
                p1_load(tt + 2)
            for _ in range(4):
                if ada_rest:
                    ada_chunk(ada_rest.pop(0))
            p1_b(tt)
        if s == 0:
            assert not ada_rest
            ada_A(2)

        def proj(bank_i, wslot, tg):
            w3 = v3(wslot.ap, KC)
            mm(bank_i, banks[bank_i][:, :], [(w3[:, kc, :], uT3[:, kc, tg * 512:(tg + 1) * 512]) for kc in range(KC)],
               reads=[wslot.buf, uT_bufs[tg]])

        AR.reset(TMP_BASE)
        diag = [(AR.alloc(31 * 128 * 2, BF16), Buf(f"diag{i}")) for i in range(2)]
        S1 = AR.alloc(T * 4, F32)
        S2 = AR.alloc(T * 4, F32)
        S_bufs = [Buf(f"S{tb}") for tb in range(NTG)]
        ysq = [(AR.alloc(512 * 2, BF16), Buf(f"ysq{i}")) for i in range(2)]
        sig = [(AR.alloc(512 * 4, F32), Buf(f"sig{i}")) for i in range(2)]
        ltmp = AR.alloc(512 * 4, F32)
        ltmp_buf = Buf("ltmp")
        ytmp = [(AR.alloc(512 * 4, F32), Buf(f"ytmp{i}")) for i in range(2)]
        convw3 = v3(convw.ap, 8)

        k = 0
        kq = 0
        pending = []

        def stats(c, tb, yq_ap, yq_buf):
            sl_ = slice(tb * 512, (tb + 1) * 512)
            mm(6, banks[6][:, :], [(onesb.ap, catT3[:, 8 + c, sl_])], reads=[cat_bufs[8 + c][tb], onesb.buf])
            mm(7, banks[7][:, :], [(onesb.ap, yq_ap)], reads=[yq_buf, onesb.buf])
            if c == 0:
                vop("dve", "tensor_copy", [bank_buf[6]], [S_bufs[tb]], S1[:, sl_], banks[6][:, :])
                vop("dve", "tensor_copy", [bank_buf[7]], [S_bufs[tb]], S2[:, sl_], banks[7][:, :])
            else:
                vop("dve", "tensor_tensor", [bank_buf[6], S_bufs[tb]], [S_bufs[tb]], S1[:, sl_], banks[6][:, :],
                    S1[:, sl_], ALU.add)
                vop("dve", "tensor_tensor", [bank_buf[7], S_bufs[tb]], [S_bufs[tb]], S2[:, sl_], banks[7][:, :],
                    S2[:, sl_], ALU.add)

        for c in range(8):
            wa = next_w4()
            load_w(wa, win_d[24 + c])
            wb = next_w4()
            load_w(wb, win_d[32 + c])
            for tg in range(NTG):
                ba, bb = 2 * (k % 2), 2 * (k % 2) + 1
                proj(ba, wa, tg)
                proj(bb, wb, tg)
                sg_ap, sg_buf = sig[k % 2]
                act(sg_ap, banks[bb][:, :], AF.Sigmoid, [bank_buf[bb]], [sg_buf])
                vop("dve", "tensor_tensor", [bank_buf[ba], sg_buf], [cat_bufs[8 + c][tg]],
                    catT3[:, 8 + c, tg * 512:(tg + 1) * 512], banks[ba][:, :], sg_ap, ALU.mult)
                k += 1
            dg, dg_buf = diag[c % 2]
            dg3 = v3(dg, 31)
            def fdv(eng, dg3=dg3, c=c):
                ins = None
                for tap in range(0, 31, 2):
                    ins = eng.tensor_scalar(dg3[:, tap, :], identf.ap, convw3[:, c, tap:tap + 1], None, ALU.mult)
                return ins

            def fac(eng, dg3=dg3, c=c):
                ins = None
                for tap in range(1, 31, 2):
                    ins = eng.activation(dg3[:, tap, :], identf.ap, AF.Copy, scale=convw3[:, c, tap:tap + 1])
                return ins

            S.op("dve", fdv, reads=[identf.buf, convw.buf], writes=[dg_buf])
            S.op("act", fac, reads=[identf.buf, convw.buf], writes=[dg_buf])
            for tb in range(NTG - 1, -1, -1):
                bi = 4 + (kq % 2)
                outs = []
                for tap in range(30, -1, -1):
                    sh = 30 - tap
                    lo = tb * 512 - sh
                    if lo >= 0:
                        outs.append((banks[bi][:, 0:512], dg3[:, tap, :], catT3[:, 8 + c, lo:lo + 512]))
                    else:
                        outs.append((banks[bi][:, sh:512], dg3[:, tap, :], catT3[:, 8 + c, 0:512 - sh]))

                def fnc(eng, outs=outs):
                    ins = None
                    for i, (o, l, r) in enumerate(outs):
                        ins = eng.matmul(o, l, r, start=(i == 0), stop=(i == len(outs) - 1))
                    return ins

                rd = [dg_buf, cat_bufs[8 + c][tb]] + ([cat_bufs[8 + c][tb - 1]] if tb > 0 else [])
                S.op("pe", fnc, reads=rd, writes=[bank_buf[bi]])
                if pending:
                    stats(*pending.pop())
                yq_ap, yq_buf = ysq[kq % 2]
                act(catT3[:, 8 + c, tb * 512:(tb + 1) * 512], banks[bi][:, :], AF.Identity,
                    [bank_buf[bi], convb.buf], [cat_bufs[8 + c][tb]], bias=convb.ap[:, c:c + 1])
                act(yq_ap, banks[bi][:, :], AF.Square, [bank_buf[bi], convb.buf], [yq_buf],
                    bias=convb.ap[:, c:c + 1])
                pending.append((c, tb, yq_ap, yq_buf))
                kq += 1
        stats(*pending.pop())
        for tb in range(NTG):
            sl_ = slice(tb * 512, (tb + 1) * 512)
            mu, var = S1[:, sl_], S2[:, sl_]
            sb_ = S_bufs[tb]
            vop("dve", "tensor_scalar", [sb_], [sb_], mu, mu, 1.0 / 1024, None, ALU.mult)
            vop("dve", "tensor_tensor", [sb_], [ltmp_buf], ltmp, mu, mu, ALU.mult)
            vop("dve", "scalar_tensor_tensor", [sb_, ltmp_buf], [sb_], var, var, 1.0 / 1024, ltmp,
                ALU.mult, ALU.subtract)
            act(var, var, AF.Sqrt, [sb_, eps_buf], [sb_], bias=eps_ap)
            vop("dve", "reciprocal", [sb_], [sb_], var, var)
            vop("dve", "scalar_tensor_tensor", [sb_], [sb_], mu, mu, -1.0, var, ALU.mult, ALU.mult)
            for c in range(8):
                yt_ap, yt_buf = ytmp[c % 2]
                vop("dve", "tensor_tensor", [cat_bufs[8 + c][tb], sb_], [yt_buf], yt_ap, catT3[:, 8 + c, sl_], var,
                    ALU.mult)
                vop("dve", "tensor_tensor", [yt_buf, sb_], [yt_buf], yt_ap, yt_ap, mu, ALU.add)
                act(catT3[:, 8 + c, sl_], yt_ap, AF.Silu, [yt_buf, lng.buf, lnb.buf],
                    [cat_bufs[8 + c][tb]], bias=lnb.ap[:, c:c + 1], scale=lng.ap[:, c:c + 1])

        S.barrier()
        AR.reset(TMP_BASE)
        cos_sl = Slot(AR.alloc(T * 4, F32), "cos")
        sin_sl = Slot(AR.alloc(T * 4, F32), "sin")
        load_plain("sp", cos_sl, cos_d)
        load_plain("sp", sin_sl, sin_d)
        qT = AR.alloc(T * 2, BF16)
        kT = AR.alloc(T * 2, BF16)
        qT_bufs = [Buf(f"qT{tg}") for tg in range(NTG)]
        kT_buf = Buf("kT")
        vsb = AR.alloc(16 * 128 * 2, BF16)
        vsb3 = v3(vsb, 16)
        v_buf = Buf("vsb")
        vT = [(AR.alloc(512 * 2, BF16), Buf(f"vT{i}")) for i in range(1)]
        rt1 = [(AR.alloc(512 * 4, F32), Buf(f"rt1_{i}")) for i in range(1)]
        rt2 = [(AR.alloc(512 * 4, F32), Buf(f"rt2_{i}")) for i in range(1)]
        pT = [(AR.alloc(512 * 2, BF16), Buf(f"pT{i}")) for i in range(3)]
        kmf = AR.alloc(8 * 4, F32)
        kmf_buf = Buf("kmf")
        kmb = AR.alloc(8 * 2, BF16)
        kmb_buf = Buf("kmb")
        g2 = AR.alloc(16 * 8 * 4, F32)
        g2_3 = v3(g2, 16)
        g2_buf = Buf("g2")
        top8 = AR.alloc(16 * 8 * 4, F32)
        top8_3 = v3(top8, 16)
        top8_buf = Buf("top8")
        selb = AR.alloc(16 * 8 * 4, F32)
        selb_3 = v3(selb, 16)
        selb_buf = Buf("selb")
        biasb = AR.alloc(16 * 8 * 2, BF16)
        biasb_3 = v3(biasb, 16)
        biasb_buf = Buf("biasb")
        biasT = [(AR.alloc(512 * 2, BF16), Buf(f"biasT{i}")) for i in range(2)]
        for bt_ap, bt_buf in biasT:
            vop("dve", "memset", [], [bt_buf], bt_ap, 0.0)
        rden = AR.alloc(512 * 4, F32)
        rden_buf = Buf("rden")
        pvs = AR.alloc(512 * 4, F32)
        pvs_buf = Buf("pvs")

        def rope(bank_i, dst_ap, dst_buf, tg, k):
            ps = banks[bank_i]
            t1, t1b = rt1[0]
            t2, t2b = rt2[0]
            cs = cos_sl.ap[:, tg * 512:(tg + 1) * 512]
            sn = sin_sl.ap[:, tg * 512:(tg + 1) * 512]
            vop("dve", "tensor_tensor", [bank_buf[bank_i], cos_sl.buf], [t1b], t1, ps[:, :], cs, ALU.mult)
            vop("dve", "tensor_tensor", [bank_buf[bank_i], sin_sl.buf], [t2b], t2[0:64, :], ps[64:128, :],
                sn[64:128, :], ALU.mult)
            vop("dve", "tensor_tensor", [bank_buf[bank_i], sin_sl.buf], [t2b], t2[64:128, :], ps[0:64, :],
                sn[0:64, :], ALU.mult)
            vop("dve", "tensor_tensor", [t1b, t2b], [dst_buf], dst_ap, t1, t2, ALU.add)

        kk = 0
        for h in range(NH):
            wq = next_w4()
            load_w(wq, win_d[h])
            wk = next_w4()
            load_w(wk, win_d[8 + h])
            wv = next_w4()
            load_w(wv, win_d[16 + h])
            for tg in range(NTG):
                sl_ = slice(tg * 512, (tg + 1) * 512)
                bi = kk % 2
                proj(bi, wq, tg)
                rope(bi, qT[:, sl_], qT_bufs[tg], tg, kk)
                kk += 1
                bi = kk % 2
                proj(bi, wk, tg)
                rope(bi, kT[:, sl_], kT_buf, tg, kk)
                kk += 1
                bi = kk % 2
                proj(bi, wv, tg)
                vt_ap, vt_buf = vT[0]
                act(vt_ap, banks[bi][:, :], AF.Copy, [bank_buf[bi]], [vt_buf])
                kk += 1
                pb = banks[7].bitcast(BF16)
                transposes(7, [(pb[:, j * 128:(j + 1) * 128], vt_ap[:, j * 128:(j + 1) * 128], identb.ap)
                               for j in range(4)], reads=[vt_buf, identb.buf])
                vop("dve", "tensor_copy", [bank_buf[7]], [v_buf],
                    vsb3[:, tg * 4:(tg + 1) * 4, :], v3(pb[:, 0:512], 4))
            vop("dve", "tensor_reduce", [kT_buf], [kmf_buf], kmf, v3(kT, 8), AX.X, ALU.add)
            vop("dve", "tensor_scalar", [kmf_buf], [kmb_buf], kmb, kmf, 1.0 / 256, None, ALU.mult)
            g_ps = banks[7][:, 0:128]

            def fng(eng, g_ps=g_ps):
                ins = None
                for qt in range(16):
                    ins = eng.matmul(g_ps[:, qt * 8:(qt + 1) * 8], qT[:, qt * 128:(qt + 1) * 128], kmb,
                                     start=True, stop=True)
                return ins

            S.op("pe", fng, reads=qT_bufs + [kmb_buf], writes=[bank_buf[7]])
            vop("dve", "tensor_tensor", [bank_buf[7], pastb.buf], [g2_buf], g2, g_ps, pastb.ap, ALU.add)
            def fmax(eng):
                ins = None
                for qt in range(16):
                    ins = eng.max(top8_3[:, qt, :], g2_3[:, qt, :])
                return ins

            S.op("dve", fmax, reads=[g2_buf], writes=[top8_buf])
            vop("dve", "tensor_tensor", [g2_buf, top8_buf], [selb_buf], selb_3, g2_3,
                top8_3[:, :, 2:3].to_broadcast([128, 16, 8]), ALU.is_ge)
            vop("dve", "tensor_scalar", [selb_buf], [selb_buf], selb, selb, 1.0, BIG, ALU.subtract, ALU.mult)
            vop("dve", "tensor_tensor", [selb_buf, ownfix.buf], [biasb_buf], biasb, selb, ownfix.ap, ALU.max)

            for qg in range(NTG):
                bt_ap, bt_buf = biasT[qg % 2]
                if qg >= 2:
                    pbt = banks[7].bitcast(BF16)
                    transposes(7, [(pbt[0:8, j * 128:(j + 1) * 128], biasb_3[:, qg * 4 + j, :], identb.ap)
                                   for j in range(4)], reads=[biasb_buf, identb.buf])
                    vop("dve", "tensor_copy", [bank_buf[7]], [bt_buf], bt_ap[0:8, :], pbt[0:8, 0:512])
                nkt = 4 * qg + 4
                qsl = slice(qg * 512, (qg + 1) * 512)

                def qk(kt):
                    sb = 2 + (kt % 3)
                    pairs = [(kT[:, kt * 128:(kt + 1) * 128], qT[:, qsl])]
                    if qg >= 2:
                        pairs.append((v3(ej.ap, 8)[:, kt // 2, :], bt_ap))
                    if kt >= 4 * qg:
                        pairs.append((identb.ap, v3(cmask.ap, 4)[:, kt - 4 * qg, :]))
                    mm(sb, banks[sb][:, :], pairs, reads=[kT_buf, qT_bufs[qg], bt_buf, ej.buf, identb.buf, cmask.buf])

                def pv(kt):
                    sb = 2 + (kt % 3)
                    p_ap, p_buf = pT[kt % 3]
                    act(p_ap, banks[sb][:, :], AF.Exp, [bank_buf[sb]], [p_buf], scale=SCALE)
                    mm(5, banks[5][:, :], [(vsb3[:, kt, :], p_ap)], reads=[v_buf, p_buf],
                       first=(kt == 0), last=(kt == nkt - 1))
                    mm(6, banks[6][:, :], [(onesb.ap, p_ap)], reads=[onesb.buf, p_buf],
                       first=(kt == 0), last=(kt == nkt - 1))

                qk(0)
                qk(1)
                for kt in range(nkt):
                    if kt + 2 < nkt:
                        qk(kt + 2)
                    pv(kt)
                act(pvs, banks[5][:, :], AF.Copy, [bank_buf[5]], [pvs_buf])
                vop("dve", "reciprocal", [bank_buf[6]], [rden_buf], rden, banks[6][:, :])
                vop("dve", "tensor_tensor", [pvs_buf, rden_buf], [cat_bufs[h][qg]],
                    catT3[:, h, qsl], pvs, rden, ALU.mult)

        S.barrier()
        AR.reset(PHASE_BASE)
        mgs = []
        for i in range(2):
            mg_ = AR.alloc(KC * 512 * 4, F32)
            mgs.append((v3(mg_, KC), [Buf(f"mg{i}_{n}") for n in range(KC)]))
        assert AR.mark() <= TMP_BASE, (AR.mark(), TMP_BASE)
        AR.reset(TMP_BASE)
        xt = [Slot(AR.alloc(D * 4, F32), f"xt4_{i}") for i in range(1)]
        h1t = [Slot(AR.alloc(D * 4, F32), f"h1t{i}") for i in range(2)]
        xs = [(AR.alloc(D * 2, BF16), Buf(f"xs4_{i}")) for i in range(2)]
        u2t = [Slot(AR.alloc(KC * 128 * 2, BF16), f"u2t{i}") for i in range(2)]
        junk4 = AR.alloc(D * 2, BF16)
        junk4_buf = Buf("junk4")

        def outproj_group(tg, n):
            mg3, mg_bufs = mgs[tg % 2]
            sl = next_w4()
            load_w(sl, wout_d[n])
            w3 = v3(sl.ap, KC)
            bi = n % 2
            mm(bi, banks[bi][:, :], [(w3[:, kc, :], catT3[:, kc, tg * 512:(tg + 1) * 512]) for kc in range(KC)],
               reads=[sl.buf] + [cat_bufs[kc][tg] for kc in range(KC)])
            act(mg3[:, n, :], banks[bi][:, :], AF.Copy, [bank_buf[bi], modT_bufs[2]], [mg_bufs[n]],
                scale=colG(1, n, s))

        for n in range(KC):
            outproj_group(0, n)
        k = 0
        for tg in range(NTG):
            mg3, mg_bufs = mgs[tg % 2]
            nxt = [(tg + 1, n) for n in range(KC)] if tg + 1 < NTG else []
            for j in range(4):
                tt = tg * 4 + j
                xsl = xt[0]
                load_plain("sp", xsl, x_d[s, tt * 128:(tt + 1) * 128, :])
                hsl = h1t[k % 2]
                for q4 in range(4):
                    bi = 2 + q4
                    transposes(bi, [(banks[bi][:, i * 128:(i + 1) * 128], mg3[:, q4 * 4 + i, j * 128:(j + 1) * 128],
                                     identf.ap) for i in range(4)],
                               reads=[mg_bufs[q4 * 4 + i] for i in range(4)] + [identf.buf])
                    vop("dve", "tensor_tensor", [bank_buf[bi], xsl.buf], [hsl.buf],
                        hsl.ap[:, q4 * 512:(q4 + 1) * 512], banks[bi][:, :], xsl.ap[:, q4 * 512:(q4 + 1) * 512], ALU.add)
                dma("sp", h1s_d[s, tt * 128:(tt + 1) * 128, :], hsl.ap, hsl.sem, reads=[hsl.buf])
                xs_ap, xs_buf = xs[k % 2]
                rs, rsb = norm_a1(hsl.ap, hsl.buf, junk4, junk4_buf)
                norm_a2(hsl.ap, hsl.buf, rs, rsb, xs_ap, xs_buf)
                for _ in range(4):
                    if nxt:
                        outproj_group(*nxt.pop(0))
                usl = u2t[k % 2]
                u3 = v3(usl.ap, KC)
                norm_b(2, s, xs_ap, xs_buf, (6, 7), lambda kc, u3=u3: u3[:, kc, :], usl.buf)
                dma("sp", u2s_d[s, tg][:, j * 2048:(j + 1) * 2048], usl.ap, usl.sem, reads=[usl.buf])
                k += 1

        S.barrier()
        AR.reset(PHASE_BASE)
        hT = AR.alloc(FC * 1024 * 2, BF16)
        hT3 = v3(hT, FC)
        hT_bufs = [[Buf(f"hT{f}_{t2}") for t2 in range(2)] for f in range(FC)]
        gfin = Slot(AR.alloc(D * 4, F32), "gfin")
        load_plain("sp", gfin, gfin_d)
        Y_BASE = AR.mark()
        for g in range(2):
            S.barrier()
            AR.reset(Y_BASE)
            u2g = [Slot(AR.alloc(KC * 512 * 2, BF16), f"u2g{i}") for i in range(2)]
            sgt = [(AR.alloc(512 * 4, F32), Buf(f"sgt{i}")) for i in range(2)]
            for t2 in range(2):
                load_plain("sp", u2g[t2], u2s_d[s, g * 2 + t2])
            k = 0
            for f in range(FC):
                wgs = next_w4()
                load_w(wgs, wg_d[f])
                wus = next_w4()
                load_w(wus, wu_d[f])
                wg3 = v3(wgs.ap, KC)
                wu3 = v3(wus.ap, KC)
                for t2 in range(2):
                    u4 = u2g[t2].ap.rearrange("p (j k t) -> p j k t", j=4, k=KC)
                    bg, bu = 2 * (k % 2), 2 * (k % 2) + 1
                    og = banks[bg][:, :].rearrange("p (j t) -> p j t", j=4)
                    ou = banks[bu][:, :].rearrange("p (j t) -> p j t", j=4)
                    mm(bg, og, [(wg3[:, kc, :], u4[:, :, kc, :]) for kc in range(KC)],
                       reads=[wgs.buf, u2g[t2].buf])
                    mm(bu, ou, [(wu3[:, kc, :], u4[:, :, kc, :]) for kc in range(KC)],
                       reads=[wus.buf, u2g[t2].buf])
                    sg_ap, sg_buf = sgt[k % 2]
                    act(sg_ap, banks[bg][:, :], AF.Silu, [bank_buf[bg]], [sg_buf])
                    vop("dve", "tensor_tensor", [bank_buf[bu], sg_buf], [hT_bufs[f][t2]],
                        hT3[:, f, t2 * 512:(t2 + 1) * 512], banks[bu][:, :], sg_ap, ALU.mult)
                    k += 1
            S.barrier()
            AR.reset(Y_BASE)
            fg = AR.alloc(KC * 512 * 4, F32)
            fg3 = v3(fg, KC)
            fg_bufs = [Buf(f"fg{n}") for n in range(KC)]
            h1l = [Slot(AR.alloc(D * 4, F32), f"h1l{i}") for i in range(2)]
            junk = AR.alloc(D * 2, BF16)
            junk_buf = Buf("junk")
            k = 0
            for t2 in range(2):
                for n in range(KC):
                    sl = next_w11()
                    load_w(sl, wd_d[n])
                    w3 = v3(sl.ap, FC)
                    bi = n % 2
                    mm(bi, banks[bi][:, :], [(w3[:, f, :], hT3[:, f, t2 * 512:(t2 + 1) * 512]) for f in range(FC)],
                       reads=[sl.buf] + [hT_bufs[f][t2] for f in range(FC)])
                    act(fg3[:, n, :], banks[bi][:, :], AF.Copy, [bank_buf[bi], modT_bufs[5]], [fg_bufs[n]],
                        scale=colG(2, n, s))
                for j in range(4):
                    tt = g * 8 + t2 * 4 + j
                    hsl = h1l[k % 2]
                    load_plain("sp", hsl, h1s_d[s, tt * 128:(tt + 1) * 128, :])
                    for q4 in range(4):
                        bi = 2 + q4
                        transposes(bi, [(banks[bi][:, i * 128:(i + 1) * 128],
                                         fg3[:, q4 * 4 + i, j * 128:(j + 1) * 128], identf.ap) for i in range(4)],
                                   reads=[fg_bufs[q4 * 4 + i] for i in range(4)] + [identf.buf])
                        vop("dve", "tensor_tensor", [bank_buf[bi], hsl.buf], [hsl.buf],
                            hsl.ap[:, q4 * 512:(q4 + 1) * 512], banks[bi][:, :], hsl.ap[:, q4 * 512:(q4 + 1) * 512],
                            ALU.add)
                    ss, ssb = new_stat()
                    act(junk, hsl.ap, AF.Square, [hsl.buf], [junk_buf, ssb], accum=ss)
                    sd, sdb = new_stat()
                    act(sd, ss, AF.Sqrt, [ssb, eps_buf], [sdb], bias=eps_ap, scale=1.0 / D)
                    rs, rsb = new_stat()
                    vop("dve", "reciprocal", [sdb], [rsb], rs, sd)
                    vop("dve", "scalar_tensor_tensor", [hsl.buf, rsb, gfin.buf], [hsl.buf],
                        hsl.ap, hsl.ap, rs, gfin.ap, ALU.mult, ALU.mult)
                    dma("sp", out_d[s, tt * 128:(tt + 1) * 128, :], hsl.ap, hsl.sem, reads=[hsl.buf])
                    k += 1

    S.barrier()

    sem_names = list(S.ENGS) + S.dma_keys
    sems = {}
    from contextlib import ExitStack
    with ExitStack() as es:
        for nm in sem_names:
            sems[nm] = es.enter_context(nc.semaphore(nm))
        block = es.enter_context(nc.Block())
        S.emit(nc, block, sems)
    return nc


def _tile_w(w, kc):
    K, N = w.shape
    return np.ascontiguousarray(w.reshape(kc, 128, N // 128, 128).transpose(2, 1, 0, 3)).reshape(N // 128, 128, kc * 128)


def _cols(v, n):
    return np.ascontiguousarray(v.reshape(n, 128).T)


_CACHE = {}


def _constants():
    if "c" in _CACHE:
        return _CACHE["c"]
    bf = ml_dtypes.bfloat16
    identf = np.eye(128, dtype=np.float32)
    identb = identf.astype(bf)
    onesb = np.ones((128, 128), bf)
    k = np.arange(128)[:, None]
    q = np.arange(512)[None, :]
    cm = np.stack([np.where(j * 128 + k > q, -BIG, 0.0) for j in range(4)], 1).astype(np.float32)
    cmask = cm.reshape(128, 4 * 512).astype(bf)
    ej = np.zeros((128, 8, 128), np.float32)
    for j in range(8):
        ej[j, j, :] = 1.0
    ej = ej.reshape(128, 8 * 128).astype(bf)
    half = 64
    inv = (np.float32(10000.0) ** (-np.arange(half, dtype=np.float32) / np.float32(half))).astype(np.float32)
    pos = np.arange(T, dtype=np.float32)
    ang = (pos[:, None] * inv[None, :]).astype(np.float32)
    cos = np.cos(ang).astype(np.float32).T
    sin = np.sin(ang).astype(np.float32).T
    cosF = np.concatenate([cos, cos], 0)
    sinS = np.concatenate([sin, -sin], 0)
    past = np.zeros((128, 16, 8), np.float32)
    own = np.full((128, 16, 8), -BIG, np.float32)
    for qt in range(16):
        blk = qt // 2
        past[:, qt, blk:] = -1e30
        own[:, qt, blk] = 0.0
    c = dict(identf=identf, identb=identb, onesb=onesb, cmask=cmask, ej=ej,
             cos=np.ascontiguousarray(cosF), sin=np.ascontiguousarray(sinS),
             pastb=past.reshape(128, 128), ownfix=own.reshape(128, 128))
    _CACHE["c"] = c
    return c


def kernel(x, c, w_ada, b_ada, g_mix, w_in, conv_w, conv_b, ln_g, ln_b, w_out, g_ffn, w_gate, w_up, w_down,
           g_final):
    f = lambda a: np.asarray(a, dtype=np.float32)
    x = f(x)
    c = f(c)
    shared = dict(
        wada=_tile_w(f(w_ada)[0], KC),
        bada=_cols(f(b_ada)[0], 96),
        gmix=_cols(f(g_mix)[0], KC),
        gffn=_cols(f(g_ffn)[0], KC),
        win=_tile_w(f(w_in)[0], KC),
        wout=_tile_w(f(w_out)[0], KC),
        wg=_tile_w(f(w_gate)[0], KC),
        wu=_tile_w(f(w_up)[0], KC),
        wd=_tile_w(f(w_down)[0], FC),
        convw=np.ascontiguousarray(f(conv_w)[0].reshape(31, 8, 128).transpose(2, 1, 0)).reshape(128, 8 * 31),
        convb=_cols(f(conv_b)[0], 8),
        lng=_cols(f(ln_g)[0], 8),
        lnb=_cols(f(ln_b)[0], 8),
        gfin=np.ascontiguousarray(np.broadcast_to(f(g_final)[None, :], (128, D))),
    )
    shared.update(_constants())
    in_maps = []
    for i in range(NCORES):
        m = dict(shared)
        m["x"] = np.ascontiguousarray(x[2 * i:2 * i + 2])
        m["cT"] = np.ascontiguousarray(c[2 * i:2 * i + 2].reshape(2, KC, 128).transpose(2, 1, 0)).reshape(128, KC * 2)
        in_maps.append(m)
    if "nc" not in _CACHE:
        _CACHE["nc"] = build_program()
    nc = _CACHE["nc"]
    res = run_bass_kernel_spmd(nc, in_maps, core_ids=list(range(NCORES)))
    out = np.concatenate([np.asarray(r["out"]) for r in res.results], axis=0)
    return out.astype(np.float32)
```
